# Optimizing a Trainium2 kernel written in Bass

```python
import math
import jax
import jax.numpy as jnp
from jax import lax
import numpy as np


D_MODEL = 1024
BATCH = 16
SEQ = 4096
DEPTH = 2

CTX_LEN = 256
GRID_W = 64
F32 = jnp.float32
BRANCH_WIDTH = 512
N_BRANCH = 3
M_HEADS = 4
M_HEAD_DIM = BRANCH_WIDTH // M_HEADS
M_WIDTH = M_HEADS * M_HEAD_DIM
M_GATES = 4 * M_HEADS
M_CHUNK = 64
M_CONV = 3
W_HEADS = 8
W_KV_HEADS = 2
W_HEAD_DIM = BRANCH_WIDTH // W_HEADS
W_WIDTH = W_HEADS * W_HEAD_DIM
W_KV_WIDTH = W_KV_HEADS * W_HEAD_DIM
WINDOW = 128
W_BLOCK = 128
DF_HEADS = 4
DF_V_DIM = BRANCH_WIDTH // DF_HEADS
DF_QK_DIM = DF_V_DIM // 2
DF_WIDTH = DF_HEADS * DF_V_DIM
DF_QK_WIDTH = DF_HEADS * 2 * DF_QK_DIM
DF_BLOCK = 128
ROPE_DIM = 64
ROPE_BASE = 10000.0
FFN_HIDDEN = -(-8 * D_MODEL // (3 * 256)) * 256
EPS = 1e-6
IN_SIZES = (M_WIDTH, M_WIDTH, M_WIDTH, M_WIDTH, M_GATES,
            W_WIDTH, W_KV_WIDTH, W_KV_WIDTH,
            DF_QK_WIDTH, DF_QK_WIDTH, DF_WIDTH,
            N_BRANCH * D_MODEL)
IN_WIDTH = sum(IN_SIZES)
IN_OFFSETS = tuple(int(o) for o in np.cumsum(IN_SIZES)[:-1])

kernel_name = 'hybrid_mlstm_swa_diffattn_prefix_block'


def rms_norm(x, g):
    xf = x.astype(F32)
    y = xf * lax.rsqrt(jnp.mean(xf * xf, axis=-1, keepdims=True) + EPS)
    return (y * g.astype(F32)).astype(x.dtype)


def head_rms_norm(y, g):
    H, dh = y.shape[-2:]
    return rms_norm(y, g.reshape(H, dh)).reshape(y.shape[:-2] + (H * dh,))


def modulate(x, g, shift, scale):
    return rms_norm(x, g) * (1 + scale) + shift


def adaln(cond, w_mod, b_mod):
    return jnp.split(jax.nn.silu(cond) @ w_mod + b_mod, 6, axis=-1)


def split_heads(a, h):
    return a.reshape(a.shape[:-1] + (h, a.shape[-1] // h))


def axial_rope_tables(n_tokens):
    rows = n_tokens // GRID_W
    r, col = jnp.meshgrid(jnp.arange(rows), jnp.arange(GRID_W), indexing='ij')
    half = ROPE_DIM // 2
    inv = ROPE_BASE ** (-jnp.arange(0, half, 2, dtype=F32) / half)
    ang_r = r.reshape(-1, 1).astype(F32) * inv
    ang_c = col.reshape(-1, 1).astype(F32) * inv
    ang = jnp.concatenate([ang_r, ang_r, ang_c, ang_c], axis=-1)
    return jnp.cos(ang), jnp.sin(ang)


def apply_rope(x, cos, sin):
    xf = x.astype(F32)
    a = xf.reshape(xf.shape[:-1] + (2, 2, ROPE_DIM // 4))
    rot = jnp.concatenate([-a[..., 1:, :], a[..., :1, :]], axis=-2).reshape(xf.shape)
    return (xf * cos[:, None, :] + rot * sin[:, None, :]).astype(x.dtype)


def short_conv(u, w):
    K = w.shape[0]
    p = K // 2
    T = u.shape[1]
    up = jnp.pad(u, ((0, 0), (p, p), (0, 0)))
    return sum(up[:, j:j + T] * w[j] for j in range(K))


def mlstm_inputs(mq, mk, mv, mg, b_gate, conv_w):
    B, T, _ = mq.shape
    qk = jax.nn.silu(short_conv(jnp.concatenate([mq, mk], axis=-1), conv_w))
    heads = lambda a: jnp.transpose(a.astype(F32).reshape(B, T, M_HEADS, M_HEAD_DIM), (0, 2, 1, 3))
    q = heads(qk[..., :M_WIDTH])
    k = heads(qk[..., M_WIDTH:]) * M_HEAD_DIM ** -0.5
    v = heads(mv)
    g = jnp.transpose((mg + b_gate).astype(F32).reshape(B, T, 4, M_HEADS), (2, 0, 3, 1))
    return q, k, v, g


def mlstm_zero_state(b):
    return (jnp.zeros((b, M_HEADS, M_HEAD_DIM, M_HEAD_DIM), F32),
            jnp.zeros((b, M_HEADS, M_HEAD_DIM), F32),
            jnp.zeros((b, M_HEADS), F32))


def mlstm_scan(q, k, v, i_pre, log_f, state, return_h):
    B, H, T, d = q.shape
    nc = T // M_CHUNK
    chunks = lambda a: jnp.moveaxis(a.reshape((B, H, nc, M_CHUNK) + a.shape[3:]), 2, 0)
    causal = jnp.tril(jnp.ones((M_CHUNK, M_CHUNK), dtype=bool))

    def step(carry, inp):
        C, n, m = carry
        qc, kc, vc, ic, fc = inp
        b = jnp.cumsum(fc, axis=-1)
        b_end = b[..., -1]
        log_end = b_end[..., None] - b + ic
        m_new = jnp.maximum(b_end + m, log_end.max(-1))
        w_end = jnp.exp(log_end - m_new[..., None])
        decay = jnp.exp(b_end + m - m_new)
        C_new = decay[..., None, None] * C + jnp.einsum('bhs,bhsv,bhsk->bhvk', w_end, vc, kc)
        n_new = decay[..., None] * n + jnp.einsum('bhs,bhsk->bhk', w_end, kc)
        if not return_h:
            return (C_new, n_new, m_new), None
        log_w = jnp.where(causal, b[..., :, None] - b[..., None, :] + ic[..., None, :], -jnp.inf)
        log_inter = b + m[..., None]
        m_row = jnp.maximum(log_inter, log_w.max(-1))
        w_intra = jnp.exp(log_w - m_row[..., None])
        w_inter = jnp.exp(log_inter - m_row)
        s = jnp.einsum('bhtk,bhsk->bhts', qc, kc) * w_intra
        num = jnp.einsum('bhts,bhsv->bhtv', s, vc) + w_inter[..., None] * jnp.einsum('bhvk,bhtk->bhtv', C, qc)
        den = s.sum(-1) + w_inter * jnp.einsum('bhk,bhtk->bht', n, qc)
        h = num / jnp.maximum(jnp.abs(den), jnp.exp(-m_row))[..., None]
        return (C_new, n_new, m_new), h

    carry, hs = lax.scan(step, state, tuple(chunks(a) for a in (q, k, v, i_pre, log_f)))
    h = jnp.moveaxis(hs, 0, 2).reshape(B, H, T, d) if return_h else None
    return h, carry


def mlstm_bidir(q, k, v, g, st_fwd, st_bwd, return_h):
    i_f, f_f, i_b, f_b = g
    h_f, st_f = mlstm_scan(q, k, v, i_f, jax.nn.log_sigmoid(f_f), st_fwd, return_h)
    rev = lambda a: jnp.flip(a, axis=2)
    h_b, st_b = mlstm_scan(rev(q), rev(k), rev(v), rev(i_b), rev(jax.nn.log_sigmoid(f_b)), st_bwd, return_h)
    h = h_f + rev(h_b) if return_h else None
    return h, st_f, st_b


def mlstm_out(h, o_pre, norm_w):
    B, H, T, d = h.shape
    y = jnp.transpose(h, (0, 2, 1, 3)) * jax.nn.sigmoid(o_pre.astype(F32)).reshape(B, T, H, d)
    return head_rms_norm(y, norm_w).astype(o_pre.dtype)


def _sink_attend(qn, key_sets, sink_l):
    logits = []
    for k, _, valid in key_sets:
        s = jnp.einsum('bqgrd,bkgd->bgrqk', qn, k)
        logits.append(s if valid is None else jnp.where(valid, s, -jnp.inf))
    m = sink_l
    for s in logits:
        m = jnp.maximum(m, s.max(-1))
    e = [jnp.exp(s - m[..., None]) for s in logits]
    den = jnp.exp(sink_l - m) + sum(ei.sum(-1) for ei in e)
    num = sum(jnp.einsum('bgrqk,bkgd->bgrqd', ei, ks[1]) for ei, ks in zip(e, key_sets))
    return jnp.transpose(num / den[..., None], (0, 3, 1, 2, 4))


def window_attention_latent(q, k, v, kc, vc, sink):
    B, T, H, d = q.shape
    G = k.shape[2]
    R = H // G
    nb = T // W_BLOCK
    qb = jnp.moveaxis(q.astype(F32).reshape(B, nb, W_BLOCK, G, R, d), 1, 0) * d ** -0.5
    pad = ((0, 0), (W_BLOCK, W_BLOCK), (0, 0), (0, 0))
    kp = jnp.pad(k.astype(F32), pad)
    vp = jnp.pad(v.astype(F32), pad)
    ctx_set = (kc.astype(F32), vc.astype(F32), None)
    sink_l = sink.astype(F32).reshape(G, R, 1)
    q_off = jnp.arange(W_BLOCK)
    k_off = jnp.arange(3 * W_BLOCK) - W_BLOCK

    def one_block(args):
        n, qn = args
        start = n * W_BLOCK
        kn = lax.dynamic_slice_in_dim(kp, start, 3 * W_BLOCK, axis=1)
        vn = lax.dynamic_slice_in_dim(vp, start, 3 * W_BLOCK, axis=1)
        s_pos = start + k_off
        valid = (jnp.abs(q_off[:, None] - k_off[None, :]) <= WINDOW) & ((s_pos >= 0) & (s_pos < T))[None, :]
        return _sink_attend(qn, ((kn, vn, valid), ctx_set), sink_l)

    out = lax.map(one_block, (jnp.arange(nb), qb))
    return jnp.moveaxis(out, 0, 1).reshape(B, T, H * d)


def window_attention_ctx(q, k, v, sink):
    B, Tc, H, d = q.shape
    G = k.shape[2]
    qn = q.astype(F32).reshape(B, Tc, G, H // G, d) * d ** -0.5
    o = _sink_attend(qn, ((k.astype(F32), v.astype(F32), None),), sink.astype(F32).reshape(G, H // G, 1))
    return o.reshape(B, Tc, H * d)


def _diff_attend(qn, k, v, lam):
    s = jnp.einsum('bqhcd,bkhcd->bchqk', qn, k)
    p = jax.nn.softmax(s, axis=-1)
    return jnp.einsum('bhqk,bkhv->bqhv', p[:, 0] - lam * p[:, 1], v)


def diff_attention_latent(q, k, v, kc, vc, lam):
    B, T, H, _, d = q.shape
    nb = T // DF_BLOCK
    k_all = jnp.concatenate([kc, k], axis=1).astype(F32)
    v_all = jnp.concatenate([vc, v], axis=1).astype(F32)
    qb = jnp.moveaxis(q.astype(F32).reshape(B, nb, DF_BLOCK, H, 2, d), 1, 0) * d ** -0.5
    out = lax.map(lambda qn: _diff_attend(qn, k_all, v_all, lam), qb)
    return jnp.moveaxis(out, 0, 1).reshape(B, T, H, -1)


def diff_out(y, d_norm, lam_init, dt):
    return (head_rms_norm(y, d_norm) * (1 - lam_init)).astype(dt)


def merge_branches(ys, gate_pre, w_branch, w_out):
    B, T, _ = gate_pre.shape
    gates = jax.nn.sigmoid(gate_pre).reshape(B, T, N_BRANCH, D_MODEL)
    merged = sum(gates[..., i, :] * (ys[i] @ w_branch[i]) for i in range(N_BRANCH))
    return merged @ w_out


def swiglu(h, w_in, w_out):
    gate, up = jnp.split(h @ w_in, 2, axis=-1)
    return (jax.nn.silu(gate) * up) @ w_out


def mixer_sublayer(h, hc, w_in, b_gate, conv_w, m_norm, sink, lam, lam_init, d_norm,
                   w_branch, w_out, cos, sin, need_ctx):
    dt = h.dtype
    B, T, _ = h.shape
    Tc = hc.shape[1]
    p = jnp.split(h @ w_in, IN_OFFSETS, axis=-1)
    pc = jnp.split(hc @ w_in, IN_OFFSETS, axis=-1)
    q, k, v, g = mlstm_inputs(p[0], p[1], p[2], p[4], b_gate, conv_w)
    qc, kc, vc, gc = mlstm_inputs(pc[0], pc[1], pc[2], pc[4], b_gate, conv_w)
    zero = mlstm_zero_state(hc.shape[0])
    hm_c, st_f, st_b = mlstm_bidir(qc, kc, vc, gc, zero, zero, need_ctx)
    hm, _, _ = mlstm_bidir(q, k, v, g, st_f, st_b, True)
    ya = mlstm_out(hm, p[3], m_norm)
    wq = apply_rope(split_heads(p[5], W_HEADS), cos, sin)
    wk = apply_rope(split_heads(p[6], W_KV_HEADS), cos, sin)
    wv = split_heads(p[7], W_KV_HEADS)
    wkc = split_heads(pc[6], W_KV_HEADS)
    wvc = split_heads(pc[7], W_KV_HEADS)
    yb = window_attention_latent(wq, wk, wv, wkc, wvc, sink).astype(dt)
    dq = apply_rope(p[8].reshape(B, T, DF_HEADS * 2, DF_QK_DIM), cos, sin).reshape(B, T, DF_HEADS, 2, DF_QK_DIM)
    dk = apply_rope(p[9].reshape(B, T, DF_HEADS * 2, DF_QK_DIM), cos, sin).reshape(B, T, DF_HEADS, 2, DF_QK_DIM)
    dv = split_heads(p[10], DF_HEADS)
    dkc = pc[9].reshape(B, Tc, DF_HEADS, 2, DF_QK_DIM)
    dvc = split_heads(pc[10], DF_HEADS)
    yc = diff_out(diff_attention_latent(dq, dk, dv, dkc, dvc, lam), d_norm, lam_init, dt)
    out = merge_branches((ya, yb, yc), p[11], w_branch, w_out)
    if not need_ctx:
        return out, None
    ya_c = mlstm_out(hm_c, pc[3], m_norm)
    yb_c = window_attention_ctx(split_heads(pc[5], W_HEADS), wkc, wvc, sink).astype(dt)
    dqc = pc[8].reshape(B, Tc, DF_HEADS, 2, DF_QK_DIM).astype(F32) * DF_QK_DIM ** -0.5
    yc_c = diff_out(_diff_attend(dqc, dkc.astype(F32), dvc.astype(F32), lam), d_norm, lam_init, dt)
    out_c = merge_branches((ya_c, yb_c, yc_c), pc[11], w_branch, w_out)
    return out, out_c


def setup_inputs(seed: int = 0) -> dict:
    key = jax.random.key(seed)
    ks = jax.random.split(key, 24)
    nrm = lambda k, shape, s: jax.random.normal(k, shape, F32) * s
    L = DEPTH
    D = D_MODEL
    fb = jnp.linspace(3.0, 6.0, M_HEADS)
    zb = jnp.zeros((M_HEADS,), F32)
    gate_base = jnp.concatenate([zb, fb, zb, fb])
    return {
        'x': nrm(ks[0], (BATCH, SEQ, D), 1.0),
        'c': nrm(ks[1], (BATCH, D), 1.0),
        'ctx': nrm(ks[2], (BATCH, CTX_LEN, D), 1.0),
        'c_ctx': nrm(ks[3], (D,), 1.0),
        'w_mod': nrm(ks[4], (L, D, 6 * D), 0.5 * D ** -0.5),
        'b_mod': nrm(ks[5], (L, 6 * D), 0.02),
        'g_mix': 1.0 + nrm(ks[6], (L, D), 0.02),
        'g_ffn': 1.0 + nrm(ks[7], (L, D), 0.02),
        'w_in': nrm(ks[8], (L, D, IN_WIDTH), D ** -0.5),
        'b_gate': gate_base + nrm(ks[9], (L, M_GATES), 0.1),
        'conv_w': nrm(ks[10], (L, M_CONV, 2 * M_WIDTH), M_CONV ** -0.5),
        'm_norm': 1.0 + nrm(ks[11], (L, M_WIDTH), 0.02),
        'sink': nrm(ks[12], (L, W_HEADS), 0.5),
        'lam_q1': nrm(ks[13], (L, DF_QK_DIM), 0.1),
        'lam_k1': nrm(ks[14], (L, DF_QK_DIM), 0.1),
        'lam_q2': nrm(ks[15], (L, DF_QK_DIM), 0.1),
        'lam_k2': nrm(ks[16], (L, DF_QK_DIM), 0.1),
        'd_norm': 1.0 + nrm(ks[17], (L, DF_WIDTH), 0.02),
        'w_branch': nrm(ks[18], (L, N_BRANCH, BRANCH_WIDTH, D), BRANCH_WIDTH ** -0.5),
        'w_out': nrm(ks[19], (L, D, D), D ** -0.5),
        'w_ffn_in': nrm(ks[20], (L, D, 2 * FFN_HIDDEN), D ** -0.5),
        'w_ffn_out': nrm(ks[21], (L, FFN_HIDDEN, D), FFN_HIDDEN ** -0.5),
        'g_final': 1.0 + nrm(ks[22], (D,), 0.02),
    }


def reference(x, c, ctx, c_ctx, w_mod, b_mod, g_mix, g_ffn, w_in, b_gate, conv_w, m_norm, sink,
              lam_q1, lam_k1, lam_q2, lam_k2, d_norm, w_branch, w_out, w_ffn_in, w_ffn_out, g_final):
    T = x.shape[1]
    cos, sin = axial_rope_tables(T)
    xc = ctx
    for l in range(DEPTH):
        need_ctx = l < DEPTH - 1
        sh1, sc1, gt1, sh2, sc2, gt2 = adaln(c[:, None, :], w_mod[l], b_mod[l])
        sh1c, sc1c, gt1c, sh2c, sc2c, gt2c = adaln(c_ctx, w_mod[l], b_mod[l])
        lam_init = 0.8 - 0.6 * math.exp(-0.3 * l)
        lam = (jnp.exp(jnp.sum(lam_q1[l].astype(F32) * lam_k1[l].astype(F32)))
               - jnp.exp(jnp.sum(lam_q2[l].astype(F32) * lam_k2[l].astype(F32))) + lam_init)
        out, out_c = mixer_sublayer(modulate(x, g_mix[l], sh1, sc1), modulate(xc, g_mix[l], sh1c, sc1c),
                                    w_in[l], b_gate[l], conv_w[l], m_norm[l], sink[l], lam, lam_init,
                                    d_norm[l], w_branch[l], w_out[l], cos, sin, need_ctx)
        x = x + gt1 * out
        x = x + gt2 * swiglu(modulate(x, g_ffn[l], sh2, sc2), w_ffn_in[l], w_ffn_out[l])
        if need_ctx:
            xc = xc + gt1c * out_c
            xc = xc + gt2c * swiglu(modulate(xc, g_ffn[l], sh2c, sc2c), w_ffn_in[l], w_ffn_out[l])
    return rms_norm(x, g_final)
```

```python
import math
import os
SKIP = os.environ.get('MK_SKIP', '').split(',')
from contextlib import ExitStack

import numpy as np
import concourse.bass as bass
import concourse.mybir as mybir
from concourse.bass_utils import run_bass_kernel_spmd

F32 = mybir.dt.float32
BF16 = mybir.dt.bfloat16
AF = mybir.ActivationFunctionType
ALU = mybir.AluOpType

D = 1024
TC = 256
HID = 2816
NCORES = 8
EPS = 1e-6
EPOCH = 30000
LN_KSCALE = math.log(128.0 ** -0.5)
NEG = -30000.0

OFF = dict(mq=0, mk=512, mv=1024, mo=1536, mg=2048, wq=2064, wk=2576, wv=2704, dq=2832, dk=3344, dv=3856, gp=4368)


def _rot_cols(c0):
    out = []
    for i in range(128):
        blk, ii = divmod(i, 64)
        pi = ii + 16 if (ii % 32) < 16 else ii - 16
        out.append(c0 + blk * 64 + pi)
    return out


def build_in_plan():
    cols = []
    fm = []
    tm = []

    def add(colidx):
        c0 = len(cols)
        cols.extend(colidx)
        return c0

    for name, row0 in (("mq", 0), ("mk", 512)):
        c0 = add(range(OFF[name], OFF[name] + 512))
        fm.append(dict(c0=c0, n=4, kind="plain", dest=("MQK", row0)))
    pairs = []
    for j in range(4):
        pairs.append((OFF["wq"] + j * 128, "q", ("WQ", j)))
    pairs.append((OFF["wk"], "k", ("WK", 0)))
    for j in range(4):
        pairs.append((OFF["dq"] + j * 128, "q", ("DQ", j)))
    for j in range(4):
        pairs.append((OFF["dk"] + j * 128, "k", ("DK", j)))
    for i in range(0, len(pairs), 2):
        grp = pairs[i:i + 2]
        colidx = []
        for (pc, tab, dest) in grp:
            colidx.extend(range(pc, pc + 128))
            colidx.extend(_rot_cols(pc))
        c0 = add(colidx)
        fm.append(dict(c0=c0, n=2 * len(grp), kind="rope", pairs=[(tab, dest) for (_, tab, dest) in grp]))
    for j in range(6):
        c0 = add(range(OFF["gp"] + j * 512, OFF["gp"] + (j + 1) * 512))
        fm.append(dict(c0=c0, n=4, kind="sig", dest=("GP", j * 512)))
    for name in ("mv", "mo", "dv"):
        c0 = add(range(OFF[name], OFF[name] + 512))
        tm.append(dict(c0=c0, w=512, kind=name))
    c0 = add(list(range(OFF["wv"], OFF["wv"] + 128)) + list(range(OFF["mg"], OFF["mg"] + 16)))
    tm.append(dict(c0=c0, w=144, kind="wvg"))
    return cols, fm, tm


IN_COLS, FM_TILES, TM_TILES = build_in_plan()
NEXT = len(IN_COLS)
NEXT_PAD = ((NEXT + 511) // 512) * 512

FFN_COLS = []
for _j in range(11):
    for _q in range(2):
        FFN_COLS.extend(range((2 * _j + _q) * 128, (2 * _j + _q + 1) * 128))
    for _q in range(2):
        FFN_COLS.extend(range(HID + (2 * _j + _q) * 128, HID + (2 * _j + _q + 1) * 128))


class Eng:
    def __init__(self, fw, name):
        self.fw = fw
        self.name = name
        self.ops = []
        self.sem = None
        self.cnt = 0
        self.seen = {}

    def new_epoch(self):
        self.sem = self.fw.new_sem()
        self.cnt = 0
        if self.name == "pe":
            self.fw.pe_sems.add(id(self.sem))


class Buf:
    def __init__(self, name="", nparts=1):
        self.name = name
        self.excl = False
        self.w = [None] * nparts
        self.r = [[] for _ in range(nparts)]


class TT_:
    def __init__(self, h, nparts=1, name=""):
        self.h = h
        self.b = Buf(name, nparts)

    def __getitem__(self, idx):
        return self.h[idx]

    def part(self, i):
        return (self.b, i)


class FW:
    def __init__(self, nc, stack, nsem=96):
        self.nc = nc
        self.sems = [stack.enter_context(nc.semaphore(f"s{i}")) for i in range(nsem)]
        self.semi = 0
        self.pe_sems = set()
        self.eng = {n: Eng(self, n) for n in ["pe", "act", "dve", "pool", "sp"]}
        for e in self.eng.values():
            e.new_epoch()
        self.dma_slots = {q: [[self.new_sem(), 0] for _ in range(8)] for q in ("sp", "pool", "act")}
        self.dma_rr = {"sp": 0, "pool": 0, "act": 0}
        self.all_dma_ev = {}

    def new_sem(self):
        s = self.sems[self.semi]
        self.semi += 1
        return s

    def _norm(self, lst):
        out = []
        for x in lst:
            if isinstance(x, TT_):
                x = x.b
            if isinstance(x, Buf):
                out.extend((x, i) for i in range(len(x.w)))
            else:
                b, i = x
                if isinstance(b, TT_):
                    b = b.b
                out.append((b, i))
        return out

    def _deps(self, reads, writes):
        deps = []
        for (b, i) in reads:
            if b.w[i] is not None:
                deps.append(b.w[i])
        for (b, i) in writes:
            if b.w[i] is not None:
                deps.append(b.w[i])
            deps.extend(b.r[i])
        return deps

    def _waits(self, e, deps):
        ws = {}
        for (sem, val) in deps:
            k = id(sem)
            if e.seen.get(k, 0) >= val:
                continue
            if k not in ws or ws[k][1] < val:
                ws[k] = (sem, val)
        for k, (sem, val) in ws.items():
            e.seen[k] = val
        return list(ws.values())

    def _record(self, ev, reads, writes):
        for (b, i) in writes:
            b.w[i] = ev
            b.r[i] = []
        for (b, i) in reads:
            b.r[i].append(ev)

    def op(self, engname, fn, R=(), W=()):
        e = self.eng[engname]
        reads = self._norm(R)
        writes = self._norm(W)
        xr = [x for x in reads if x[0].excl]
        if xr:
            reads = [x for x in reads if not x[0].excl]
            writes = writes + xr
        deps = self._deps(reads, writes)
        if engname == "pe":
            pe_sems = self.pe_sems
            deps = [d_ for d_ in deps if id(d_[0]) not in pe_sems]
        waits = self._waits(e, deps)
        if e.cnt >= EPOCH:
            e.new_epoch()
        e.cnt += 1
        sem = e.sem
        ev = (sem, e.cnt)

        def run(h, fn=fn, waits=waits, sem=sem):
            for (s, v) in waits:
                h.wait_ge(s, v)
            fn(h).then_inc(sem, 1)
        e.ops.append(run)
        self._record(ev, reads, writes)
        return ev

    def dma(self, engname, out, in_, R=(), W=(), **kw):
        e = self.eng[engname]
        reads = self._norm(R)
        writes = self._norm(W)
        slots = self.dma_slots[engname]
        slot = slots[self.dma_rr[engname]]
        self.dma_rr[engname] = (self.dma_rr[engname] + 1) % len(slots)
        deps = self._deps(reads, writes)
        if slot[1] > 0:
            deps.append((slot[0], slot[1]))
        if slot[1] >= EPOCH:
            slot[0] = self.new_sem()
            slot[1] = 0
        waits = self._waits(e, deps)
        slot[1] += 16
        sem = slot[0]
        ev = (sem, slot[1])
        self.all_dma_ev[id(sem)] = ev

        def run(h, waits=waits, sem=sem, out=out, in_=in_, kw=kw):
            for (s, v) in waits:
                h.wait_ge(s, v)
            h.dma_start(out=out, in_=in_, **kw).then_inc(sem, 16)
        e.ops.append(run)
        self._record(ev, reads, writes)
        return ev

    def barrier(self):
        evs = list(self.all_dma_ev.values())
        for e in self.eng.values():
            if e.cnt > 0:
                evs.append((e.sem, e.cnt))
        for e in self.eng.values():
            waits = self._waits(e, evs)

            def run(h, waits=waits):
                for (s, v) in waits:
                    h.wait_ge(s, v)
            e.ops.append(run)

    def emit(self, block):
        m = {"pe": block.tensor, "act": block.scalar, "dve": block.vector, "pool": block.gpsimd, "sp": block.sync}
        for n, dec in m.items():
            ops = self.eng[n].ops

            def body(h, ops=ops):
                for f in ops:
                    f(h)
            dec(body)


class Prog:
    def __init__(self, TL, NB, depth=2, debug=False, stop_after=None):
        self.TL, self.NB, self.L = TL, NB, depth
        self.TT = TC + TL
        self.NT = self.TT // 128
        self.NC = NB + 1
        self.debug = debug
        self.stop_after = stop_after
        self.nc = bass.Bass("TRN2", target_bir_lowering=False)
        self.bank_rr = 0
        self.tslot = 0

    def dram_in(self, name, shape, dt=F32):
        return self.nc.dram_tensor(name, list(shape), dt, kind="ExternalInput").ap()

    def scratch(self, name, shape, dt):
        kind = "ExternalOutput" if self.debug else "Internal"
        return self.nc.dram_tensor(name, list(shape), dt, kind=kind).ap()

    def sb(self, st, name, shape, dt, nparts=1):
        self.uid = getattr(self, "uid", 0) + 1
        name = f"{name}_u{self.uid}"
        h = st.enter_context(self.nc.sbuf_tensor(name, list(shape), dt))
        return TT_(h, nparts, name)

    def bank(self):
        b = self.banks[self.bank_rr]
        self.bank_rr = (self.bank_rr + 1) % len(self.banks)
        return b

    def next_tbank(self):
        t = self.tbanks[self.tslot % 2]
        self.tslot += 1
        return t

    def blocks(self, layer, with_ctx=True):
        out = []
        if with_ctx:
            out.append((0, TC, "ctx"))
        for t0 in range(TC, self.TT, 512):
            out.append((t0, min(512, self.TT - t0), "lat"))
        return out

    def build(self):
        nc = self.nc
        L, NB, TT, NT, NC = self.L, self.NB, self.TT, self.NT, self.NC
        I = {}
        I["xin"] = self.dram_in("xin", [NB, TT, D])
        I["cond"] = self.dram_in("cond", [NC, D])
        I["w_mod"] = self.dram_in("w_mod", [L, D, 6 * D])
        I["b_mod"] = self.dram_in("b_mod", [L, 6 * D])
        I["g_mix"] = self.dram_in("g_mix", [L, D])
        I["g_ffn"] = self.dram_in("g_ffn", [L, D])
        I["w_in"] = self.dram_in("w_in", [L, D, NEXT_PAD])
        I["b_gate"] = self.dram_in("b_gate", [L, 16])
        I["conv_w"] = self.dram_in("conv_w", [L, 3 * D])
        I["m_norm"] = self.dram_in("m_norm", [L, 512])
        I["sink"] = self.dram_in("sink", [L, 8])
        I["lam"] = self.dram_in("lam", [L, 4 * 64])
        I["d_norm"] = self.dram_in("d_norm", [L, 512])
        I["w_branch"] = self.dram_in("w_branch", [L, 1536, D])
        I["w_out"] = self.dram_in("w_out", [L, D, D])
        I["w_ffn_in"] = self.dram_in("w_ffn_in", [L, D, 2 * HID])
        I["w_ffn_out"] = self.dram_in("w_ffn_out", [L, HID, D])
        I["g_final"] = self.dram_in("g_final", [D])
        I["tabs"] = self.dram_in("tabs", [4, 128, TT])
        I["cmat"] = self.dram_in("cmat", [128, 4, 128])
        self.I = I
        self.out = nc.dram_tensor("out", [NB, self.TL, D], F32, kind="ExternalOutput").ap()

        S = {}
        S["w_in"] = self.scratch("s_w_in", [L, D, NEXT_PAD], BF16)
        S["w_branch"] = self.scratch("s_w_branch", [L, 1536, D], BF16)
        S["w_out"] = self.scratch("s_w_out", [L, D, D], BF16)
        S["w_ffn_in"] = self.scratch("s_w_ffn_in", [L, D, 2 * HID], BF16)
        S["w_ffn_out"] = self.scratch("s_w_ffn_out", [L, HID, D], BF16)
        S["gtD"] = self.scratch("s_gtD", [L, NC, 2 * D], F32)
        for n, shp, dt in (("MQK", [D, TT], BF16), ("QKC", [D, TT], BF16), ("MV", [TT, 512], BF16),
                           ("MO", [TT, 512], BF16), ("MG", [TT, 16], F32), ("WQ", [128, 4, TT], BF16),
                           ("WK", [128, TT], BF16), ("WV", [TT, 128], BF16), ("DQ", [512, TT], BF16),
                           ("DK", [512, TT], BF16), ("DV", [TT, 512], BF16), ("GP", [3072, TT], BF16),
                           ("YT", [1536, TT], BF16), ("X1", [TT, D], F32), ("XS", [TT, D], F32)):
            S[n] = [self.scratch(f"s_{n}{b}", shp, dt) for b in range(NB)]
        self.S = S

        with ExitStack() as st:
            self.fw = fw = FW(nc, st)
            self.banks = [TT_(st.enter_context(nc.psum_tensor(f"pb{i}", [128, 512], F32)), 1, f"pb{i}") for i in range(6)]
            self.tbanks = [TT_(st.enter_context(nc.psum_tensor(f"pbt{i}", [128, 1024], BF16)), 1, f"pbt{i}") for i in range(2)]
            for t_ in self.banks + self.tbanks:
                t_.b.excl = True
            self.cm = self.sb(st, "cm", [128, 4, 128], F32)
            self.identb = self.sb(st, "identb", [128, 128], BF16)
            self.mneg = self.sb(st, "mneg", [128, 2, 512], BF16)
            self.colv = self.sb(st, "colv", [128, L, 48 + 8 + 8 + 8 + 24], F32)
            self.modT = self.sb(st, "modT", [128, L, 48, NC], F32)
            self.G = self.sb(st, "G", [128, L, 2, 8, NC], F32)
            self.onesrow = self.sb(st, "onesrow", [1, 128], F32)
            self.lamc = self.sb(st, "lamc", [128, L, 2], F32)
            self.prologue(st)
            if self.stop_after != "prologue":
                done = False
                for l in range(L):
                    for b in range(NB):
                        self.phase_in(l, b)
                        if self.stop_after == "in":
                            done = True
                            break
                        self.phase_m(l, b)
                        if self.stop_after == "m":
                            done = True
                            break
                        self.phase_w(l, b)
                        if self.stop_after == "w":
                            done = True
                            break
                        self.phase_d(l, b)
                        if self.stop_after == "d":
                            done = True
                            break
                        self.phase_merge(l, b)
                        if self.stop_after == "merge":
                            done = True
                            break
                        self.phase_ffn(l, b)
                        if self.stop_after == "ffn":
                            done = True
                            break
                    if done:
                        break
            fw.barrier()
            with nc.Block() as block:
                fw.emit(block)
        return nc

    def prologue(self, st0):
        nc, fw, I, S = self.nc, self.fw, self.I, self.S
        L, NC = self.L, self.NC
        fw.dma("sp", self.cm[:], I["cmat"], W=[self.cm])
        fw.op("dve", lambda h: h.tensor_copy(out=self.identb[:], in_=self.cm[:, 0, :]), R=[self.cm], W=[self.identb])
        for i, src in ((0, 2), (1, 1)):
            for r in range(4):
                fw.op("dve", lambda h, i=i, src=src, r=r: h.tensor_scalar(
                    out=self.mneg[:, i, r * 128:(r + 1) * 128], in0=self.cm[:, src, :], scalar1=-1.0, scalar2=-NEG,
                    op0=ALU.add, op1=ALU.mult), R=[self.cm], W=[self.mneg])
        fw.op("dve", lambda h: h.memset(self.onesrow[:], 1.0), W=[self.onesrow])
        for name, rows, cols in (("w_in", D, NEXT_PAD), ("w_branch", 1536, D), ("w_out", D, D),
                                 ("w_ffn_in", D, 2 * HID), ("w_ffn_out", HID, D)):
            if 'casts' in SKIP:
                break
            for l in range(L):
                for r0 in range(0, rows, 128):
                    for c0 in range(0, cols, 2048):
                        c1 = min(cols, c0 + 2048)
                        fw.dma("pool", S[name][l, r0:r0 + 128, c0:c1], I[name][l, r0:r0 + 128, c0:c1])
        with ExitStack() as st:
            rows = self.sb(st, "rows", [1, 6144 + 1024 * 2 + 512 * 2 + 3072], F32)
            condT = self.sb(st, "condT", [128, 8, NC], F32)
            scT = self.sb(st, "scT", [128, 8, NC], F32)
            wm = [self.sb(st, f"wm{i}", [128, 8, 512], F32) for i in range(2)]
            gtrow = self.sb(st, "gtrow", [NC, 2048], F32)
            lamb = self.sb(st, "lamb", [128, 256], F32)
            lamt = self.sb(st, "lamt", [128, 4], F32)
            junk = self.sb(st, "junkp", [128, 64], F32)
            for c in range(NC):
                fw.dma("sp", condT[:, :, c:c + 1], I["cond"][c].rearrange("(kc p o) -> p kc o", p=128, o=1), W=[condT],
                       allow_slow_non_contiguous=True)
            fw.op("act", lambda h: h.activation(out=scT[:], in_=condT[:], func=AF.Silu), R=[condT], W=[scT])
            for l in range(L):
                if 'adaln' in SKIP:
                    break
                o = 0
                segs = {}
                for nm, n in (("b_mod", 6144), ("g_mix", 1024), ("g_ffn", 1024), ("m_norm", 512), ("d_norm", 512), ("conv_w", 3072)):
                    fw.dma("sp", rows[0:1, o:o + n], I[nm][l:l + 1, :], W=[rows])
                    segs[nm] = o
                    o += n
                pb = self.bank()
                j = 0
                for nm, n in (("g_mix", 8), ("g_ffn", 8), ("m_norm", 4), ("d_norm", 4), ("conv_w", 24)):
                    for q in range(n):
                        a = segs[nm] + q * 128
                        fw.op("pe", lambda h, a=a, j=j, pb=pb: h.matmul(pb[:, j:j + 1], lhsT=rows[0:1, a:a + 128], rhs=self.onesrow[0:1, 0:1],
                                                                        start=True, stop=True), R=[rows, self.onesrow], W=[pb])
                        j += 1
                fw.op("dve", lambda h, l=l, pb=pb, j=j: h.tensor_copy(out=self.colv[:, l, 48:48 + j], in_=pb[:, 0:j]), R=[pb], W=[self.colv])
                lam_init = 0.8 - 0.6 * math.exp(-0.3 * l)
                fw.op("dve", lambda h, l=l, li=lam_init: h.tensor_scalar(out=self.colv[:, l, 68:72], in0=self.colv[:, l, 68:72], scalar1=1.0 - li,
                                                                         scalar2=None, op0=ALU.mult), R=[self.colv], W=[self.colv])
                pm = self.bank()
                for t in range(12):
                    w = wm[t % 2]
                    fw.dma("sp", w[:], I["w_mod"][l].rearrange("(kc p) n -> p kc n", p=128)[:, :, t * 512:(t + 1) * 512], W=[w])
                    for q in range(4):
                        oc = t * 4 + q
                        for kc in range(8):
                            fw.op("pe", lambda h, w=w, q=q, kc=kc, oc=oc, pm=pm: h.matmul(
                                pm[:, oc * NC:(oc + 1) * NC], lhsT=w[:, kc, q * 128:(q + 1) * 128], rhs=scT[:, kc, :],
                                start=(kc == 0), stop=False), R=[w, scT], W=[pm])
                        a = segs["b_mod"] + oc * 128
                        fw.op("pe", lambda h, a=a, oc=oc, pm=pm: h.matmul(pm[:, oc * NC:(oc + 1) * NC], lhsT=rows[0:1, a:a + 128],
                                                                          rhs=self.onesrow[0:1, 0:NC], start=False, stop=True),
                              R=[rows, self.onesrow], W=[pm])
                    if t in (4, 5, 10, 11):
                        pr = self.bank()
                        for kc in range(8):
                            fw.op("pe", lambda h, w=w, kc=kc, pr=pr: h.matmul(pr[0:NC, :], lhsT=scT[:, kc, :], rhs=w[:, kc, :],
                                                                              start=(kc == 0), stop=False), R=[w, scT], W=[pr])
                        a = segs["b_mod"] + t * 512
                        fw.op("pe", lambda h, a=a, pr=pr: h.matmul(pr[0:NC, :], lhsT=self.onesrow[0:1, 0:NC], rhs=rows[0:1, a:a + 512],
                                                                   start=False, stop=True), R=[rows, self.onesrow], W=[pr])
                        co = {4: 0, 5: 512, 10: 1024, 11: 1536}[t]
                        fw.op("act", lambda h, pr=pr, co=co: h.activation(out=gtrow[:, co:co + 512], in_=pr[0:NC, :], func=AF.Copy),
                              R=[pr], W=[gtrow])
                fw.op("dve", lambda h, l=l, pm=pm: h.tensor_copy(out=self.modT[:, l].rearrange("p a c -> p (a c)"), in_=pm[:, 0:48 * NC]),
                      R=[pm], W=[self.modT])
                fw.dma("pool", S["gtD"][l], gtrow[:], R=[gtrow])
                for s, (gc, scc) in enumerate(((48, 8), (56, 32))):
                    fw.op("dve", lambda h, l=l, s=s, scc=scc: h.tensor_scalar(out=self.G[:, l, s], in0=self.modT[:, l, scc:scc + 8, :], scalar1=1.0,
                                                                              scalar2=None, op0=ALU.add), R=[self.modT], W=[self.G])
                    fw.op("dve", lambda h, l=l, s=s, gc=gc: h.tensor_tensor(
                        out=self.G[:, l, s], in0=self.G[:, l, s],
                        in1=self.colv[:, l, gc:gc + 8].unsqueeze(2).broadcast_to([128, 8, NC]), op=ALU.mult), R=[self.G, self.colv], W=[self.G])
                fw.dma("sp", lamb[:], I["lam"][l].partition_broadcast(128), W=[lamb])
                for q in range(2):
                    fw.op("dve", lambda h, q=q: h.tensor_tensor(out=junk[:], in0=lamb[:, q * 128:q * 128 + 64], in1=lamb[:, q * 128 + 64:q * 128 + 128],
                                                                op=ALU.mult), R=[lamb], W=[junk])
                    fw.op("dve", lambda h, q=q: h.tensor_reduce(out=lamt[:, q:q + 1], in_=junk[:], axis=mybir.AxisListType.X, op=ALU.add),
                          R=[junk], W=[lamt])
                fw.op("act", lambda h: h.activation(out=lamt[:, 2:4], in_=lamt[:, 0:2], func=AF.Exp), R=[lamt], W=[lamt])
                fw.op("dve", lambda h, l=l, li=lam_init: h.scalar_tensor_tensor(
                    out=self.lamc[:, l, 0:1], in0=lamt[:, 2:3], scalar=li, in1=lamt[:, 3:4], op0=ALU.add, op1=ALU.subtract),
                    R=[lamt], W=[self.lamc])
                fw.op("dve", lambda h, l=l: h.tensor_scalar(out=self.lamc[:, l, 1:2], in0=self.lamc[:, l, 0:1], scalar1=-1.0, scalar2=None,
                                                            op0=ALU.mult), R=[self.lamc], W=[self.lamc])
            fw.barrier()

    def norm_mod_T(self, xt, hT, col0, l, s, ci, tmp):
        fw = self.fw
        junk, ss, rstd, xn = tmp
        shc = 0 if s == 0 else 24
        fw.op("act", lambda h: h.activation(out=junk[:], in_=xt[:], func=AF.Square, accum_out=ss[:, 0:1]), R=[xt], W=[junk, ss])
        fw.op("act", lambda h: h.activation(out=ss[:, 1:2], in_=ss[:, 0:1], func=AF.Sqrt, bias=self.epsc[:, 0:1], scale=1.0 / D), R=[ss, self.epsc], W=[ss])
        fw.op("dve", lambda h: h.reciprocal(out=rstd[:], in_=ss[:, 1:2]), R=[ss], W=[rstd])
        fw.op("act", lambda h: h.activation(out=xn[:], in_=xt[:], func=AF.Identity, scale=rstd[:, 0:1]), R=[xt, rstd], W=[xn])
        tb = self.next_tbank()
        for kc in range(8):
            fw.op("pe", lambda h, kc=kc: h.transpose(out=tb[:, kc * 128:(kc + 1) * 128], in_=xn[:, kc * 128:(kc + 1) * 128], identity=self.identb[:]),
                  R=[xn, self.identb], W=[tb])
        for kc in range(8):
            fw.op("act", lambda h, kc=kc: h.activation(
                out=hT[:, kc, col0:col0 + 128], in_=tb[:, kc * 128:(kc + 1) * 128], func=AF.Identity,
                bias=self.modT[:, l, shc + kc, ci:ci + 1], scale=self.G[:, l, s, kc, ci:ci + 1]),
                R=[tb, self.modT, self.G], W=[hT])

    def cond_index(self, b, kind):
        return self.NB if kind == "ctx" else b

    def phase_in(self, l, b):
        nc, fw, I, S = self.nc, self.fw, self.I, self.S
        TT = self.TT
        xsrc = I["xin"][b] if l == 0 else S["XS"][b]
        wsrc = S["w_in"][l].rearrange("(kc p) n -> p kc n", p=128)
        with ExitStack() as st:
            self.epsc = self.sb(st, "epsc", [128, 1], F32)
            fw.op("dve", lambda h: h.memset(self.epsc[:], EPS), W=[self.epsc])
            xts = [self.sb(st, f"xt{i}", [128, D], F32) for i in range(2)]
            junk = self.sb(st, "junk", [128, D], BF16)
            ss = self.sb(st, "ss", [128, 2], F32)
            rstd = self.sb(st, "rstd", [128, 1], F32)
            xns = [self.sb(st, f"xn{i}", [128, D], BF16) for i in range(2)]
            hTs = [self.sb(st, f"hT{i}", [128, 8, 512], BF16) for i in range(2)]
            wts = [self.sb(st, f"wt{i}", [128, 8, 512], BF16) for i in range(3)]
            stg = [self.sb(st, f"stg{i}", [128, 4, 512], BF16) for i in range(2)]
            stt = [self.sb(st, f"stt{i}", [128, 512], BF16) for i in range(3)]
            stgf = [self.sb(st, f"stgf{i}", [128, 16], F32) for i in range(2)]
            tab = [self.sb(st, f"tab{i}", [128, 4, 512], F32) for i in range(2)]
            r1 = [self.sb(st, f"r1_{i}", [128, 512], F32) for i in range(2)]
            r2 = [self.sb(st, f"r2_{i}", [128, 512], F32) for i in range(2)]
            cnt = dict(x=0, w=0, stg=0, stt=0, r=0, g=0)
            for bi, (t0, ntok, kind) in enumerate(self.blocks(l)):
                ci = self.cond_index(b, kind)
                hT = hTs[bi % 2]
                tb_ = tab[bi % 2]
                fw.dma("sp", tb_[:, :, 0:ntok], I["tabs"].rearrange("k p t -> p k t")[:, :, t0:t0 + ntok], W=[tb_])
                for m in range(ntok // 128):
                    xt = xts[cnt["x"] % 2]
                    xn = xns[cnt["x"] % 2]
                    cnt["x"] += 1
                    fw.dma("sp", xt[:], xsrc[t0 + m * 128:t0 + (m + 1) * 128, :], W=[xt])
                    self.norm_mod_T(xt, hT, m * 128, l, 0, ci, (junk, ss, rstd, xn))
                for tile in FM_TILES:
                    if 'fm' in SKIP or tile["kind"] in SKIP:
                        continue
                    wt = wts[cnt["w"] % 3]
                    cnt["w"] += 1
                    n = tile["n"]
                    fw.dma("sp", wt[:, :, 0:n * 128], wsrc[:, :, tile["c0"]:tile["c0"] + n * 128], W=[wt])
                    sg = stg[cnt["stg"] % 2]
                    cnt["stg"] += 1
                    pbs = []
                    for q in range(n):
                        pb = self.bank()
                        pbs.append(pb)
                        for kc in range(8):
                            fw.op("pe", lambda h, pb=pb, wt=wt, q=q, kc=kc, hT=hT, ntok=ntok: h.matmul(
                                pb[:, 0:ntok], lhsT=wt[:, kc, q * 128:(q + 1) * 128], rhs=hT[:, kc, 0:ntok], start=(kc == 0), stop=(kc == 7)),
                                R=[wt, hT], W=[pb])
                        if tile["kind"] in ("plain", "sig"):
                            fn = AF.Copy if tile["kind"] == "plain" else AF.Sigmoid
                            fw.op("act", lambda h, pb=pb, sg=sg, q=q, fn=fn, ntok=ntok: h.activation(out=sg[:, q, 0:ntok], in_=pb[:, 0:ntok], func=fn),
                                  R=[pb], W=[sg])
                        elif q % 2 == 1:
                            tabn, dest = tile["pairs"][q // 2]
                            ko = 0 if tabn == "q" else 2
                            ra = r1[cnt["r"] % 2]
                            rb = r2[cnt["r"] % 2]
                            cnt["r"] += 1
                            pa, pbb = pbs[q - 1], pbs[q]
                            fw.op("dve", lambda h, pa=pa, ra=ra, ko=ko, tb_=tb_, ntok=ntok: h.tensor_tensor(
                                out=ra[:, 0:ntok], in0=pa[:, 0:ntok], in1=tb_[:, ko, 0:ntok], op=ALU.mult), R=[pa, tb_], W=[ra])
                            fw.op("dve", lambda h, pbb=pbb, rb=rb, ko=ko, tb_=tb_, ntok=ntok: h.tensor_tensor(
                                out=rb[:, 0:ntok], in0=pbb[:, 0:ntok], in1=tb_[:, ko + 1, 0:ntok], op=ALU.mult), R=[pbb, tb_], W=[rb])
                            fw.op("pool", lambda h, ra=ra, rb=rb, sg=sg, q=q, ntok=ntok: h.tensor_tensor(
                                out=sg[:, q // 2, 0:ntok], in0=ra[:, 0:ntok], in1=rb[:, 0:ntok], op=ALU.add), R=[ra, rb], W=[sg])
                    if tile["kind"] in ("plain", "sig"):
                        dn, row0 = tile["dest"]
                        dst = S[dn][b][row0:row0 + 512, :].rearrange("(c p) t -> p c t", p=128)[:, :, t0:t0 + ntok]
                        fw.dma("pool", dst, sg[:, 0:4, 0:ntok], R=[sg])
                    else:
                        for pi, (tabn, (dn, j)) in enumerate(tile["pairs"]):
                            if dn == "WQ":
                                g = j // 2
                                for hh in range(2):
                                    fw.dma("pool", S["WQ"][b][g * 64:(g + 1) * 64, 2 * (j % 2) + hh, t0:t0 + ntok],
                                           sg[hh * 64:(hh + 1) * 64, pi, 0:ntok], R=[sg])
                            elif dn == "WK":
                                fw.dma("pool", S["WK"][b][:, t0:t0 + ntok], sg[:, pi, 0:ntok], R=[sg])
                            else:
                                fw.dma("pool", S[dn][b][j * 128:(j + 1) * 128, t0:t0 + ntok], sg[:, pi, 0:ntok], R=[sg])
                for tile in TM_TILES:
                    if 'tm' in SKIP or tile["kind"] in SKIP:
                        continue
                    wt = wts[cnt["w"] % 3]
                    cnt["w"] += 1
                    w = tile["w"]
                    fw.dma("sp", wt[:, :, 0:w], wsrc[:, :, tile["c0"]:tile["c0"] + w], W=[wt])
                    for m in range(ntok // 128):
                        pb = self.bank()
                        for kc in range(8):
                            fw.op("pe", lambda h, pb=pb, wt=wt, kc=kc, hT=hT, m=m, w=w: h.matmul(
                                pb[:, 0:w], lhsT=hT[:, kc, m * 128:(m + 1) * 128], rhs=wt[:, kc, 0:w], start=(kc == 0), stop=(kc == 7)),
                                R=[wt, hT], W=[pb])
                        rows = slice(t0 + m * 128, t0 + (m + 1) * 128)
                        so = stt[cnt["stt"] % 3]
                        cnt["stt"] += 1
                        if tile["kind"] == "wvg":
                            sf = stgf[cnt["g"] % 2]
                            cnt["g"] += 1
                            fw.op("dve", lambda h, pb=pb, so=so: h.tensor_copy(out=so[:, 0:128], in_=pb[:, 0:128]), R=[pb], W=[so])
                            fw.op("dve", lambda h, pb=pb, sf=sf: h.tensor_copy(out=sf[:], in_=pb[:, 128:144]), R=[pb], W=[sf])
                            fw.dma("pool", S["WV"][b][rows, :], so[:, 0:128], R=[so])
                            fw.dma("pool", S["MG"][b][rows, :], sf[:], R=[sf])
                        else:
                            eng = "act" if m % 2 == 0 else "dve"
                            if eng == "act":
                                fw.op("act", lambda h, pb=pb, so=so: h.activation(out=so[:], in_=pb[:], func=AF.Copy), R=[pb], W=[so])
                            else:
                                fw.op("dve", lambda h, pb=pb, so=so: h.tensor_copy(out=so[:], in_=pb[:]), R=[pb], W=[so])
                            dn = {"mv": "MV", "mo": "MO", "dv": "DV"}[tile["kind"]]
                            fw.dma("pool", S[dn][b][rows, :], so[:], R=[so])
            fw.barrier()

    def phase_m(self, l, b):
        nc, fw, I, S = self.nc, self.fw, self.I, self.S
        TT, NT = self.TT, self.NT
        with ExitStack() as st:
            u = self.sb(st, "cu", [128, TT], BF16)
            acc = self.sb(st, "cacc", [128, TT], F32)
            o = self.sb(st, "co", [128, TT], BF16)
            segs = [(0, TC), (TC, TT)]
            for ch in range(8):
                if 'm1' in SKIP:
                    break
                fw.dma("sp", u[:], S["MQK"][b][ch * 128:(ch + 1) * 128, :], W=[u])
                cw = lambda tap, ch=ch: self.colv[:, l, 72 + tap * 8 + ch:72 + tap * 8 + ch + 1]
                fw.op("dve", lambda h, cw=cw: h.tensor_scalar(out=acc[:], in0=u[:], scalar1=cw(1), scalar2=None, op0=ALU.mult),
                      R=[u, self.colv], W=[acc])
                for (a0, a1) in segs:
                    fw.op("dve", lambda h, cw=cw, a0=a0, a1=a1: h.scalar_tensor_tensor(
                        out=acc[:, a0 + 1:a1], in0=u[:, a0:a1 - 1], scalar=cw(0), in1=acc[:, a0 + 1:a1], op0=ALU.mult, op1=ALU.add),
                        R=[u, self.colv, acc], W=[acc])
                    fw.op("dve", lambda h, cw=cw, a0=a0, a1=a1: h.scalar_tensor_tensor(
                        out=acc[:, a0:a1 - 1], in0=u[:, a0 + 1:a1], scalar=cw(2), in1=acc[:, a0:a1 - 1], op0=ALU.mult, op1=ALU.add),
                        R=[u, self.colv, acc], W=[acc])
                fw.op("act", lambda h: h.activation(out=o[:], in_=acc[:], func=AF.Silu), R=[acc], W=[o])
                fw.dma("pool", S["QKC"][b][ch * 128:(ch + 1) * 128, :], o[:], R=[o])
            fw.barrier()
        with ExitStack() as st:
            g = self.sb(st, "mg", [128, NT, 16], F32)
            bgb = self.sb(st, "bgb", [128, 16], F32)
            e1 = self.sb(st, "me1", [128, NT, 2, 4], F32)
            SP = self.sb(st, "mSP", [128, NT, 2, 4], F32)
            Wt = self.sb(st, "mWt", [128, 2, NT, 4], F32)
            E = self.sb(st, "mE", [128, NT, 2, 4], F32)
            U = self.sb(st, "mU", [128, 2, NT, 4], F32)
            lnk = self.sb(st, "lnk", [128, 1], F32)
            H = self.sb(st, "mH", [128, NT, 512], BF16, nparts=NT)
            fw.op("dve", lambda h: h.memset(lnk[:], LN_KSCALE), W=[lnk])
            if 'm2' in SKIP:
                fw.barrier()
                return
            fw.dma("sp", bgb[:], I["b_gate"][l].partition_broadcast(128), W=[bgb])
            one = self.sb(st, "onec", [128, 1], F32)
            fw.op("dve", lambda h: h.memset(one[:], 1.0), W=[one])
            g4 = lambda: g[:].rearrange("p c (t h) -> p c t h", h=4)
            GS = 6
            if os.environ.get('MK_RR'):
                self.bank_rr = int(os.environ['MK_RR'])
            for c0 in range(0, NT, GS):
                c1 = min(NT, c0 + GS)
                n = c1 - c0
                for a0 in range(c0, c1, 2):
                    a1 = min(c1, a0 + 2)
                    fw.dma("sp", g[:, a0:a1, :], S["MG"][b][a0 * 128:a1 * 128, :].rearrange("(c p) g -> p c g", p=128), W=[g])
                fw.op("dve", lambda h, c0=c0, c1=c1, n=n: h.tensor_tensor(out=g[:, c0:c1, :], in0=g[:, c0:c1, :],
                                                                       in1=bgb[:].unsqueeze(1).broadcast_to([128, n, 16]), op=ALU.add),
                      R=[g, bgb], W=[g])
                for d in range(2):
                    fw.op("act", lambda h, d=d, c0=c0, c1=c1: h.activation(out=e1[:, c0:c1, d, :], in_=g4()[:, c0:c1, 2 * d + 1, :], func=AF.Exp, scale=-1.0),
                          R=[g], W=[e1])
                fw.op("act", lambda h, c0=c0, c1=c1: h.activation(out=SP[:, c0:c1], in_=e1[:, c0:c1], func=AF.Ln, bias=one[:, 0:1]), R=[e1, one], W=[SP])
                pg = self.bank()
                fw.op("pe", lambda h, pg=pg, c0=c0, c1=c1, n=n: h.matmul(pg[:, 0:n * 4], lhsT=self.cm[:, 1, :], rhs=SP[:, c0:c1, 0, :], start=True, stop=True),
                      R=[self.cm, SP], W=[pg])
                fw.op("pe", lambda h, pg=pg, c0=c0, c1=c1, n=n: h.matmul(pg[:, n * 4:n * 8], lhsT=self.cm[:, 2, :], rhs=SP[:, c0:c1, 1, :], start=True, stop=True),
                      R=[self.cm, SP], W=[pg])
                fw.op("pe", lambda h, pg=pg, c0=c0, c1=c1, n=n: h.matmul(pg[:, 256:256 + n * 8], lhsT=self.cm[:, 3, :],
                                                                       rhs=SP[:, c0:c1].rearrange("p c d h -> p (c d h)"), start=True, stop=True),
                      R=[self.cm, SP], W=[pg])
                fw.op("act", lambda h, pg=pg, c0=c0, c1=c1, n=n: h.activation(out=E[:, c0:c1].rearrange("p c d h -> p (c d h)"), in_=pg[:, 256:256 + n * 8],
                                                                            func=AF.Exp, scale=-1.0), R=[pg], W=[E])
                for d in range(2):
                    pv = lambda pg=pg, d=d, n=n: pg[:, d * n * 4:(d + 1) * n * 4].rearrange("p (c k) -> p c k", k=4)
                    fw.op("act", lambda h, pv=pv, d=d, c0=c0, c1=c1: h.activation(out=Wt[:, d, c0:c1, :], in_=pv(), func=AF.Exp, scale=-1.0), R=[pg], W=[Wt])
                    fw.op("dve", lambda h, pv=pv, d=d, c0=c0, c1=c1: h.tensor_tensor(out=U[:, d, c0:c1, :], in0=pv(), in1=g4()[:, c0:c1, 2 * d, :], op=ALU.add),
                          R=[pg, g], W=[U])
                    fw.op("act", lambda h, d=d, c0=c0, c1=c1: h.activation(out=U[:, d, c0:c1, :], in_=U[:, d, c0:c1, :], func=AF.Exp, bias=lnk[:, 0:1]),
                          R=[U, lnk], W=[U])

            qk = [self.sb(st, f"mqk{i}", [128, 8, 128], BF16) for i in range(2)]
            vt = [self.sb(st, f"mv{i}", [128, 512], BF16) for i in range(2)]
            ot = [self.sb(st, f"mo{i}", [128, 512], BF16) for i in range(2)]
            Sm = [self.sb(st, f"mSm{i}", [128, 128], BF16) for i in range(2)]
            kt = [self.sb(st, f"mkt{i}", [128, 128], BF16) for i in range(2)]
            vx = [self.sb(st, f"mvx{i}", [128, 129], BF16) for i in range(2)]
            nd = [self.sb(st, f"mnd{i}", [128, 129], F32) for i in range(2)]
            dn = [self.sb(st, f"mdn{i}", [128, 2], F32) for i in range(2)]
            Cf = [self.sb(st, f"mCf{h_}", [128, 129], F32) for h_ in range(4)]
            CfE = [self.sb(st, f"mCfE{h_}", [128, 129], F32) for h_ in range(4)]
            Cb = [self.sb(st, f"mCb{h_}", [128, 129], BF16) for h_ in range(4)]
            hs = [self.sb(st, f"mhs{i}", [128, 512], F32) for i in range(2)]
            so = self.sb(st, "mso", [128, 512], F32)
            yv = self.sb(st, "myv", [128, 512], F32)
            yb = self.sb(st, "myb", [128, 512], BF16)
            jk = self.sb(st, "mjk", [128, 128], F32)
            s4 = self.sb(st, "ms4", [128, 8], F32)
            epsc = self.sb(st, "mepsc", [128, 1], F32)
            ystg = [self.sb(st, f"mys{i}", [128, 4, 128], BF16) for i in range(2)]
            fw.op("dve", lambda h: h.memset(epsc[:], EPS), W=[epsc])
            qsrc = S["QKC"][b].rearrange("(j d) t -> d j t", d=128)
            step = 0
            for d in range(2):
                if 'm3' in SKIP:
                    break
                order = list(range(NT)) if d == 0 else [1, 0] + list(range(NT - 1, 1, -1))
                for h_ in range(4):
                    fw.op("dve", lambda h, h_=h_: h.memset(Cf[h_][:], 0.0), W=[Cf[h_]])
                    fw.op("dve", lambda h, h_=h_: h.memset(CfE[h_][:], 0.0), W=[CfE[h_]])
                    fw.op("dve", lambda h, h_=h_: h.memset(Cb[h_][:], 0.0), W=[Cb[h_]])
                for oi, c in enumerate(order):
                    q_ = qk[oi % 2]
                    v_ = vt[oi % 2]
                    fw.dma("sp", q_[:], qsrc[:, :, c * 128:(c + 1) * 128], W=[q_])
                    fw.dma("sp", v_[:], S["MV"][b][c * 128:(c + 1) * 128, :], W=[v_])
                    if d == 1:
                        o_ = ot[oi % 2]
                        fw.dma("sp", o_[:], S["MO"][b][c * 128:(c + 1) * 128, :], W=[o_])
                    last = oi == len(order) - 1
                    hs_ = hs[oi % 2]
                    for h_ in range(4):
                        step += 1
                        sm, k_, vx_, nd_, dn_ = Sm[step % 2], kt[step % 2], vx[step % 2], nd[step % 2], dn[step % 2]
                        col = d * 4 + h_
                        pS = self.bank()
                        fw.op("pe", lambda h, pS=pS, q_=q_, h_=h_: h.matmul(pS[:, 0:128], lhsT=q_[:, 4 + h_, :], rhs=q_[:, h_, :], start=True, stop=True),
                              R=[q_], W=[pS])
                        fw.op("dve", lambda h, pS=pS, sm=sm, d=d: h.tensor_tensor(out=sm[:], in0=pS[:, 0:128], in1=self.cm[:, 1 + d, :], op=ALU.mult),
                              R=[pS, self.cm], W=[sm])
                        fw.op("dve", lambda h, vx_=vx_, v_=v_, h_=h_, c=c, d=d: h.tensor_scalar(
                            out=vx_[:, 0:128], in0=v_[:, h_ * 128:(h_ + 1) * 128], scalar1=U[:, d, c, h_:h_ + 1], scalar2=None, op0=ALU.mult),
                            R=[v_, U], W=[vx_])
                        fw.op("pool", lambda h, vx_=vx_, c=c, d=d, h_=h_: h.tensor_copy(out=vx_[:, 128:129], in_=U[:, d, c, h_:h_ + 1]), R=[U], W=[vx_])
                        pN = self.bank()
                        fw.op("pe", lambda h, pN=pN, q_=q_, h_=h_: h.matmul(pN[:, 0:129], lhsT=q_[:, h_, :], rhs=Cb[h_][:], start=True, stop=False),
                              R=[q_, Cb[h_]], W=[pN])
                        fw.op("pe", lambda h, pN=pN, sm=sm, vx_=vx_: h.matmul(pN[:, 0:129], lhsT=sm[:], rhs=vx_[:], start=False, stop=True),
                              R=[sm, vx_], W=[pN])
                        fw.op("act", lambda h, pN=pN, nd_=nd_, c=c, d=d, h_=h_: h.activation(out=nd_[:], in_=pN[:, 0:129], func=AF.Identity,
                                                                                                scale=Wt[:, d, c, h_:h_ + 1]), R=[pN, Wt], W=[nd_])
                        fw.op("dve", lambda h, nd_=nd_, dn_=dn_: h.scalar_tensor_tensor(out=dn_[:, 0:1], in0=nd_[:, 128:129], scalar=-1.0, in1=nd_[:, 128:129],
                                                                                        op0=ALU.mult, op1=ALU.max), R=[nd_], W=[dn_])
                        fw.op("dve", lambda h, dn_=dn_: h.tensor_scalar(out=dn_[:, 0:1], in0=dn_[:, 0:1], scalar1=1.0, scalar2=None, op0=ALU.max),
                              R=[dn_], W=[dn_])
                        fw.op("dve", lambda h, dn_=dn_: h.reciprocal(out=dn_[:, 1:2], in_=dn_[:, 0:1]), R=[dn_], W=[dn_])
                        if d == 0:
                            fw.op("dve", lambda h, nd_=nd_, dn_=dn_, c=c, h_=h_: h.tensor_scalar(
                                out=H[:, c, h_ * 128:(h_ + 1) * 128], in0=nd_[:, 0:128], scalar1=dn_[:, 1:2], scalar2=None, op0=ALU.mult),
                                R=[nd_, dn_], W=[H.part(c)])
                        else:
                            fw.op("dve", lambda h, nd_=nd_, dn_=dn_, c=c, h_=h_, hs_=hs_: h.scalar_tensor_tensor(
                                out=hs_[:, h_ * 128:(h_ + 1) * 128], in0=nd_[:, 0:128], scalar=dn_[:, 1:2], in1=H[:, c, h_ * 128:(h_ + 1) * 128],
                                op0=ALU.mult, op1=ALU.add), R=[nd_, dn_, H.part(c)], W=[hs_])
                        if not last:
                            cn = order[oi + 1]
                            tb = self.next_tbank()
                            fw.op("pe", lambda h, tb=tb, q_=q_, h_=h_: h.transpose(out=tb[:, 0:128], in_=q_[:, 4 + h_, :], identity=self.identb[:]),
                                  R=[q_, self.identb], W=[tb])
                            fw.op("act", lambda h, tb=tb, k_=k_: h.activation(out=k_[:], in_=tb[:, 0:128], func=AF.Copy),
                                  R=[tb], W=[k_])
                            pC = self.bank()
                            fw.op("pe", lambda h, pC=pC, k_=k_, vx_=vx_: h.matmul(pC[:, 0:129], lhsT=k_[:], rhs=vx_[:], start=True, stop=True),
                                  R=[k_, vx_], W=[pC])
                            fw.op("dve", lambda h, pC=pC, c=c, d=d, h_=h_: h.scalar_tensor_tensor(
                                out=Cf[h_][:], in0=pC[:, 0:129], scalar=E[:, c, d, h_:h_ + 1], in1=CfE[h_][:], op0=ALU.mult, op1=ALU.add),
                                R=[pC, E, CfE[h_]], W=[Cf[h_]])
                            fw.op("act", lambda h, h_=h_: h.activation(out=Cb[h_][:], in_=Cf[h_][:], func=AF.Copy), R=[Cf[h_]], W=[Cb[h_]])
                            fw.op("act", lambda h, h_=h_, cn=cn, d=d: h.activation(out=CfE[h_][:], in_=Cf[h_][:], func=AF.Identity,
                                                                                     scale=E[:, cn, d, h_:h_ + 1]), R=[Cf[h_], E], W=[CfE[h_]])
                    if d == 1:
                        fw.op("act", lambda h, o_=o_: h.activation(out=so[:], in_=o_[:], func=AF.Sigmoid), R=[o_], W=[so])
                        fw.op("dve", lambda h, hs_=hs_: h.tensor_tensor(out=yv[:], in0=hs_[:], in1=so[:], op=ALU.mult), R=[hs_, so], W=[yv])
                        self.head_norm_store(yv, yb, jk, s4, epsc, ystg[oi % 2], 4, 128, lambda j: self.colv[:, l, 64 + j:65 + j],
                                             S["YT"][b], 0, c)
            fw.barrier()

    def head_norm_store(self, yv, yb, jk, s4, epsc, ys, nh, hd, scale_fn, ytdst, row0, c):
        fw = self.fw
        for h_ in range(nh):
            fw.op("act", lambda h, h_=h_: h.activation(out=jk[:, 0:hd], in_=yv[:, h_ * hd:(h_ + 1) * hd], func=AF.Square, accum_out=s4[:, h_:h_ + 1]),
                  R=[yv], W=[jk, s4])
        fw.op("act", lambda h: h.activation(out=s4[:, 4:4 + nh], in_=s4[:, 0:nh], func=AF.Sqrt, bias=epsc[:, 0:1], scale=1.0 / hd), R=[s4, epsc], W=[s4])
        fw.op("dve", lambda h: h.reciprocal(out=s4[:, 0:nh], in_=s4[:, 4:4 + nh]), R=[s4], W=[s4])
        fw.op("dve", lambda h: h.tensor_tensor(out=yb[:].rearrange("p (a d) -> p a d", d=hd), in0=yv[:].rearrange("p (a d) -> p a d", d=hd),
                                               in1=s4[:, 0:nh].unsqueeze(2).broadcast_to([128, nh, hd]), op=ALU.mult), R=[yv, s4], W=[yb])
        self.transpose_store(yb, ys, scale_fn, ytdst, row0, c)

    def transpose_store(self, yb, ys, scale_fn, ytdst, row0, c):
        fw = self.fw
        tb = self.next_tbank()
        for j in range(4):
            fw.op("pe", lambda h, j=j: h.transpose(out=tb[:, j * 128:(j + 1) * 128], in_=yb[:, j * 128:(j + 1) * 128], identity=self.identb[:]),
                  R=[yb, self.identb], W=[tb])
        if scale_fn is None:
            fw.op("act", lambda h: h.activation(out=ys[:].rearrange("p j t -> p (j t)"), in_=tb[:, 0:512], func=AF.Copy), R=[tb], W=[ys])
        else:
            for j in range(4):
                fw.op("act", lambda h, j=j: h.activation(out=ys[:, j, :], in_=tb[:, j * 128:(j + 1) * 128], func=AF.Identity, scale=scale_fn(j)),
                      R=[tb, self.colv], W=[ys])
        fw.dma("pool", ytdst[row0:row0 + 512, :].rearrange("(j p) t -> p j t", p=128)[:, :, c * 128:(c + 1) * 128], ys[:], R=[ys])

    def phase_w(self, l, b):
        nc, fw, I, S = self.nc, self.fw, self.I, self.S
        TT, NT = self.TT, self.NT
        with ExitStack() as st:
            K = self.sb(st, "wK", [128, TT], BF16)
            Vx = self.sb(st, "wVx", [128, NT, 2, 65], BF16)
            sk = self.sb(st, "wsk", [128, 8], F32)
            Q = [self.sb(st, f"wQ{i}", [128, 4, 128], BF16) for i in range(2)]
            P = [self.sb(st, f"wP{i}", [128, 512], BF16) for i in range(6)]
            dn = [self.sb(st, f"wdn{i}", [128, 8], F32) for i in range(2)]
            yb = [self.sb(st, f"wyb{i}", [128, 512], BF16) for i in range(2)]
            ys = [self.sb(st, f"wys{i}", [128, 4, 128], BF16) for i in range(2)]
            fw.dma("sp", K[:], S["WK"][b], W=[K])
            fw.op("dve", lambda h: h.memset(Vx[:, :, :, 64:65], 1.0), W=[Vx])
            for c0 in range(0, NT, 8):
                c1 = min(NT, c0 + 8)
                for g in range(2):
                    fw.dma("sp", Vx[:, c0:c1, g, 0:64], S["WV"][b][c0 * 128:c1 * 128, g * 64:(g + 1) * 64].rearrange("(c p) d -> p c d", p=128), W=[Vx])
            fw.dma("sp", sk[:], I["sink"][l].partition_broadcast(128), W=[sk])
            fw.op("act", lambda h: h.activation(out=sk[:], in_=sk[:], func=AF.Exp), R=[sk], W=[sk])
            pcount = 0
            tiles = range(NT) if l < self.L - 1 else range(2, NT)
            for qi, c in enumerate(tiles):
                q_ = Q[qi % 2]
                fw.dma("sp", q_[:], S["WQ"][b][:, :, c * 128:(c + 1) * 128], W=[q_])
                kts = [(0, None), (1, None)]
                if c >= 2:
                    if c - 1 >= 2:
                        kts.append((c - 1, 0))
                    kts.append((c, None))
                    if c + 1 < NT:
                        kts.append((c + 1, 1))
                yb_ = yb[qi % 2]
                for g in range(2):
                    Ps = []
                    for (kt_, mk) in kts:
                        pS = self.bank()
                        fw.op("pe", lambda h, pS=pS, g=g, kt_=kt_, q_=q_, mk=mk: h.matmul(
                            pS[:], lhsT=K[g * 64:(g + 1) * 64, kt_ * 128:(kt_ + 1) * 128], rhs=q_[g * 64:(g + 1) * 64, :, :],
                            start=True, stop=(mk is None)), R=[K, q_], W=[pS])
                        if mk is not None:
                            fw.op("pe", lambda h, pS=pS, mk=mk: h.matmul(pS[:], lhsT=self.identb[:], rhs=self.mneg[:, mk, :], start=False, stop=True),
                                  R=[self.identb, self.mneg], W=[pS])
                        p_ = P[pcount % 6]
                        pcount += 1
                        fw.op("act", lambda h, pS=pS, p_=p_: h.activation(out=p_[:], in_=pS[:], func=AF.Exp), R=[pS], W=[p_])
                        Ps.append(p_)
                    pO = self.bank()
                    for r in range(4):
                        for i, (kt_, mk) in enumerate(kts):
                            fw.op("pe", lambda h, pO=pO, r=r, i=i, kt_=kt_, g=g, Ps=Ps, nk=len(kts): h.matmul(
                                pO[:, r * 65:(r + 1) * 65], lhsT=Ps[i][:, r * 128:(r + 1) * 128], rhs=Vx[:, kt_, g, :],
                                start=(i == 0), stop=(i == nk - 1)), R=[Ps[i], Vx], W=[pO])
                    dn_ = dn[g]
                    pO3 = lambda pO=pO: pO[:, 0:260].rearrange("p (r e) -> p r e", e=65)
                    fw.op("dve", lambda h, pO3=pO3, dn_=dn_, g=g: h.tensor_tensor(out=dn_[:, 0:4], in0=pO3()[:, :, 64], in1=sk[:, g * 4:(g + 1) * 4], op=ALU.add),
                          R=[pO, sk], W=[dn_])
                    fw.op("dve", lambda h, dn_=dn_: h.reciprocal(out=dn_[:, 4:8], in_=dn_[:, 0:4]), R=[dn_], W=[dn_])
                    fw.op("dve", lambda h, pO3=pO3, dn_=dn_, g=g, yb_=yb_: h.tensor_tensor(
                        out=yb_[:, g * 256:(g + 1) * 256].rearrange("p (r e) -> p r e", e=64), in0=pO3()[:, :, 0:64],
                        in1=dn_[:, 4:8].unsqueeze(2).broadcast_to([128, 4, 64]), op=ALU.mult), R=[pO, dn_], W=[yb_])
                self.transpose_store(yb_, ys[qi % 2], None, S["YT"][b], 512, c)
            fw.barrier()

    def phase_d(self, l, b):
        nc, fw, I, S = self.nc, self.fw, self.I, self.S
        TT, NT = self.TT, self.NT
        with ExitStack() as st:
            K = self.sb(st, "dK", [128, 4, TT], BF16)
            Vx = self.sb(st, "dVx", [128, NT, 4, 129], BF16)
            Q = [self.sb(st, f"dQ{i}", [128, 4, 512], BF16) for i in range(2)]
            P = [self.sb(st, f"dP{i}", [128, 512], BF16) for i in range(3)]
            t0b = self.sb(st, "dt0", [128, 4, 128], F32)
            yd = [self.sb(st, f"dyd{m}", [128, 512], F32) for m in range(4)]
            rc = self.sb(st, "drc", [128, 8], F32)
            yb = self.sb(st, "dyb", [128, 512], BF16)
            jk = self.sb(st, "djk", [128, 128], F32)
            s4 = self.sb(st, "ds4", [128, 8], F32)
            epsc = self.sb(st, "depsc", [128, 1], F32)
            ys = [self.sb(st, f"dys{i}", [128, 4, 128], BF16) for i in range(2)]
            fw.op("dve", lambda h: h.memset(epsc[:], EPS), W=[epsc])
            for j in range(4):
                fw.dma("sp", K[:, j, :], S["DK"][b][j * 128:(j + 1) * 128, :], W=[K])
            fw.op("dve", lambda h: h.memset(Vx[:, :, :, 128:129], 1.0), W=[Vx])
            for c in range(NT):
                fw.dma("sp", Vx[:, c, :, 0:128], S["DV"][b][c * 128:(c + 1) * 128, :].rearrange("p (h d) -> p h d", h=4), W=[Vx])
            Ob = self.banks[0:4]
            Sb = self.banks[4:6]
            scount = 0
            ycount = 0
            for bi, (t0, ntok, kind) in enumerate(self.blocks(l, with_ctx=(l < self.L - 1))):
                q_ = Q[bi % 2]
                fw.dma("sp", q_[:, :, 0:ntok], S["DQ"][b].rearrange("(j p) t -> p j t", p=128)[:, :, t0:t0 + ntok], W=[q_])
                kts = [0, 1] if kind == "ctx" else list(range(NT))
                nm = ntok // 128
                for h_ in range(4):
                    for cp in range(2):
                        for i, kt_ in enumerate(kts):
                            pS = Sb[scount % 2]
                            p_ = P[scount % 3]
                            scount += 1
                            fw.op("pe", lambda h, pS=pS, cp=cp, h_=h_, kt_=kt_, q_=q_, ntok=ntok: h.matmul(
                                pS[:, 0:ntok], lhsT=K[cp * 64:(cp + 1) * 64, h_, kt_ * 128:(kt_ + 1) * 128], rhs=q_[cp * 64:(cp + 1) * 64, h_, 0:ntok],
                                start=True, stop=True), R=[K, q_], W=[pS])
                            fw.op("act", lambda h, pS=pS, p_=p_, ntok=ntok: h.activation(out=p_[:, 0:ntok], in_=pS[:, 0:ntok], func=AF.Exp), R=[pS], W=[p_])
                            for m in range(nm):
                                fw.op("pe", lambda h, m=m, p_=p_, kt_=kt_, h_=h_, i=i, kts=kts: h.matmul(
                                    Ob[m][:, 0:129], lhsT=p_[:, m * 128:(m + 1) * 128], rhs=Vx[:, kt_, h_, :], start=(i == 0), stop=(i == len(kts) - 1)),
                                    R=[p_, Vx], W=[Ob[m]])
                        for m in range(nm):
                            fw.op("dve", lambda h, m=m: h.reciprocal(out=rc[:, m:m + 1], in_=Ob[m][:, 128:129]), R=[Ob[m]], W=[rc])
                            if cp == 0:
                                fw.op("dve", lambda h, m=m: h.tensor_scalar(out=t0b[:, m, :], in0=Ob[m][:, 0:128], scalar1=rc[:, m:m + 1], scalar2=None, op0=ALU.mult),
                                      R=[Ob[m], rc], W=[t0b])
                            else:
                                fw.op("dve", lambda h, m=m: h.tensor_tensor(out=rc[:, 4 + m:5 + m], in0=rc[:, m:m + 1], in1=self.lamc[:, l, 1:2], op=ALU.mult),
                                      R=[rc, self.lamc], W=[rc])
                                fw.op("dve", lambda h, m=m, h_=h_: h.scalar_tensor_tensor(
                                    out=yd[m][:, h_ * 128:(h_ + 1) * 128], in0=Ob[m][:, 0:128], scalar=rc[:, 4 + m:5 + m], in1=t0b[:, m, :],
                                    op0=ALU.mult, op1=ALU.add), R=[Ob[m], rc, t0b], W=[yd[m]])
                for m in range(nm):
                    c = t0 // 128 + m
                    self.head_norm_store(yd[m], yb, jk, s4, epsc, ys[ycount % 2], 4, 128, lambda j: self.colv[:, l, 68 + j:69 + j],
                                         S["YT"][b], 1024, c)
                    ycount += 1
            fw.barrier()

    def phase_merge(self, l, b):
        nc, fw, I, S = self.nc, self.fw, self.I, self.S
        NB = self.NB
        xsrc = I["xin"][b] if l == 0 else S["XS"][b]
        with ExitStack() as st:
            WB = self.sb(st, "gWB", [128, 12, D], BF16)
            WO = self.sb(st, "gWO", [128, 8, D], BF16)
            gtB = self.sb(st, "ggt", [128, 2, D], F32)
            Y = [self.sb(st, f"gY{i}", [128, 12, 512], BF16) for i in range(2)]
            GP = [self.sb(st, f"gGP{i}", [128, 8, 512], BF16) for i in range(3)]
            T = [self.sb(st, f"gT{i}", [128, 512], F32) for i in range(6)]
            mgT = self.sb(st, "gmgT", [128, 8, 512], BF16)
            xt = [self.sb(st, f"gx{i}", [128, D], F32) for i in range(2)]
            tm = [self.sb(st, f"gtm{i}", [128, 512], F32) for i in range(2)]
            for k3 in range(3):
                fw.dma("sp", WB[:, k3 * 4:(k3 + 1) * 4, :], S["w_branch"][l][k3 * 512:(k3 + 1) * 512, :].rearrange("(kc p) n -> p kc n", p=128), W=[WB])
            for k2 in range(2):
                fw.dma("sp", WO[:, k2 * 4:(k2 + 1) * 4, :], S["w_out"][l][k2 * 512:(k2 + 1) * 512, :].rearrange("(kc p) n -> p kc n", p=128), W=[WO])
            fw.dma("sp", gtB[:, 0, :], S["gtD"][l, b, 0:D].partition_broadcast(128), W=[gtB])
            fw.dma("sp", gtB[:, 1, :], S["gtD"][l, NB, 0:D].partition_broadcast(128), W=[gtB])
            gcount = 0
            xcount = 0
            for bi, (t0, ntok, kind) in enumerate(self.blocks(l, with_ctx=(l < self.L - 1))):
                gi = 1 if kind == "ctx" else 0
                y_ = Y[bi % 2]
                for k3 in range(3):
                    fw.dma("sp", y_[:, k3 * 4:(k3 + 1) * 4, 0:ntok],
                           S["YT"][b][k3 * 512:(k3 + 1) * 512, :].rearrange("(j p) t -> p j t", p=128)[:, :, t0:t0 + ntok], W=[y_])
                gps = []
                for i in range(3):
                    gp_ = GP[i]
                    fw.dma("sp", gp_[:, :, 0:ntok], S["GP"][b][i * 1024:(i + 1) * 1024, :].rearrange("(j p) t -> p j t", p=128)[:, :, t0:t0 + ntok], W=[gp_])
                    gps.append(gp_)
                for oc in range(8):
                    Ts = T[(oc % 2) * 3:(oc % 2) * 3 + 3]
                    for i in range(3):
                        pb = self.bank()
                        for kc in range(4):
                            fw.op("pe", lambda h, pb=pb, i=i, kc=kc, oc=oc, y_=y_, ntok=ntok: h.matmul(
                                pb[:, 0:ntok], lhsT=WB[:, i * 4 + kc, oc * 128:(oc + 1) * 128], rhs=y_[:, i * 4 + kc, 0:ntok], start=(kc == 0), stop=(kc == 3)),
                                R=[WB, y_], W=[pb])
                        fw.op("dve", lambda h, pb=pb, i=i, oc=oc, Ts=Ts, gps=gps, ntok=ntok: h.tensor_tensor(
                            out=Ts[i][:, 0:ntok], in0=pb[:, 0:ntok], in1=gps[i][:, oc, 0:ntok], op=ALU.mult), R=[pb, gps[i]], W=[Ts[i]])
                    fw.op("pool", lambda h, Ts=Ts, ntok=ntok: h.tensor_tensor(out=Ts[0][:, 0:ntok], in0=Ts[0][:, 0:ntok], in1=Ts[1][:, 0:ntok], op=ALU.add),
                          R=[Ts[0], Ts[1]], W=[Ts[0]])
                    fw.op("pool", lambda h, Ts=Ts, oc=oc, ntok=ntok: h.tensor_tensor(out=mgT[:, oc, 0:ntok], in0=Ts[0][:, 0:ntok], in1=Ts[2][:, 0:ntok], op=ALU.add),
                          R=[Ts[0], Ts[2]], W=[mgT])
                for m in range(ntok // 128):
                    x_ = xt[xcount % 2]
                    xcount += 1
                    rows = slice(t0 + m * 128, t0 + (m + 1) * 128)
                    fw.dma("sp", x_[:], xsrc[rows, :], W=[x_])
                    for oc2 in range(2):
                        pb = self.bank()
                        t_ = tm[oc2]
                        for kc in range(8):
                            fw.op("pe", lambda h, pb=pb, kc=kc, m=m, oc2=oc2: h.matmul(
                                pb[:], lhsT=mgT[:, kc, m * 128:(m + 1) * 128], rhs=WO[:, kc, oc2 * 512:(oc2 + 1) * 512], start=(kc == 0), stop=(kc == 7)),
                                R=[mgT, WO], W=[pb])
                        fw.op("dve", lambda h, pb=pb, t_=t_, gi=gi, oc2=oc2: h.tensor_tensor(out=t_[:], in0=pb[:], in1=gtB[:, gi, oc2 * 512:(oc2 + 1) * 512], op=ALU.mult),
                              R=[pb, gtB], W=[t_])
                        fw.op("pool", lambda h, t_=t_, x_=x_, oc2=oc2: h.tensor_tensor(out=x_[:, oc2 * 512:(oc2 + 1) * 512], in0=x_[:, oc2 * 512:(oc2 + 1) * 512],
                                                                                      in1=t_[:], op=ALU.add), R=[t_, x_], W=[x_])
                    fw.dma("pool", S["X1"][b][rows, :], x_[:], R=[x_])
            fw.barrier()

    def phase_ffn(self, l, b):
        nc, fw, I, S = self.nc, self.fw, self.I, self.S
        NB = self.NB
        lastl = (l == self.L - 1)
        with ExitStack() as st:
            self.epsc = self.sb(st, "fepsc", [128, 1], F32)
            fw.op("dve", lambda h: h.memset(self.epsc[:], EPS), W=[self.epsc])
            W2 = self.sb(st, "fW2", [128, 22, D], BF16)
            W1 = [self.sb(st, f"fW1_{i}", [128, 8, 512], BF16) for i in range(3)]
            gtB = self.sb(st, "fgt", [128, 2, D], F32)
            gfB = self.sb(st, "fgf", [128, D], F32)
            xs = [self.sb(st, f"fx{i}", [128, D], F32) for i in range(4)]
            junk = self.sb(st, "fjunk", [128, D], BF16)
            ss = self.sb(st, "fss", [128, 2], F32)
            rstd = self.sb(st, "frstd", [128, 1], F32)
            xn = [self.sb(st, f"fxn{i}", [128, D], BF16) for i in range(2)]
            hT = [self.sb(st, f"fhT{i}", [128, 8, 512], BF16) for i in range(2)]
            A = self.sb(st, "fA", [128, 22, 512], BF16)
            sg = [self.sb(st, f"fsg{i}", [128, 512], BF16) for i in range(2)]
            tm = [self.sb(st, f"ftm{i}", [128, 512], F32) for i in range(2)]
            ob = [self.sb(st, f"fob{i}", [128, D], F32) for i in range(2)]
            for k in range(0, 22, 4):
                k1 = min(22, k + 4)
                fw.dma("sp", W2[:, k:k1, :], S["w_ffn_out"][l][k * 128:k1 * 128, :].rearrange("(kc p) n -> p kc n", p=128), W=[W2])
            fw.dma("sp", gtB[:, 0, :], S["gtD"][l, b, D:2 * D].partition_broadcast(128), W=[gtB])
            fw.dma("sp", gtB[:, 1, :], S["gtD"][l, NB, D:2 * D].partition_broadcast(128), W=[gtB])
            fw.dma("sp", gfB[:], I["g_final"].partition_broadcast(128), W=[gfB])
            wsrc = S["w_ffn_in"][l].rearrange("(kc p) n -> p kc n", p=128)
            wc = 0
            sc = 0
            oc_ = 0
            for bi, (t0, ntok, kind) in enumerate(self.blocks(l, with_ctx=(not lastl))):
                gi = 1 if kind == "ctx" else 0
                ci = self.cond_index(b, kind)
                h_T = hT[bi % 2]
                nm = ntok // 128
                xb_ = xs
                for m in range(nm):
                    fw.dma("sp", xb_[m][:], S["X1"][b][t0 + m * 128:t0 + (m + 1) * 128, :], W=[xb_[m]])
                    self.norm_mod_T(xb_[m], h_T, m * 128, l, 1, ci, (junk, ss, rstd, xn[m % 2]))
                for j in range(11):
                    w_ = W1[wc % 3]
                    wc += 1
                    fw.dma("sp", w_[:], wsrc[:, :, j * 512:(j + 1) * 512], W=[w_])
                    for q in range(2):
                        hc = 2 * j + q
                        pg, pu = self.bank(), self.bank()
                        for kc in range(8):
                            fw.op("pe", lambda h, pg=pg, w_=w_, q=q, kc=kc, h_T=h_T, ntok=ntok: h.matmul(
                                pg[:, 0:ntok], lhsT=w_[:, kc, q * 128:(q + 1) * 128], rhs=h_T[:, kc, 0:ntok], start=(kc == 0), stop=(kc == 7)),
                                R=[w_, h_T], W=[pg])
                        for kc in range(8):
                            fw.op("pe", lambda h, pu=pu, w_=w_, q=q, kc=kc, h_T=h_T, ntok=ntok: h.matmul(
                                pu[:, 0:ntok], lhsT=w_[:, kc, 256 + q * 128:256 + (q + 1) * 128], rhs=h_T[:, kc, 0:ntok], start=(kc == 0), stop=(kc == 7)),
                                R=[w_, h_T], W=[pu])
                        s_ = sg[sc % 2]
                        sc += 1
                        fw.op("act", lambda h, pg=pg, s_=s_, ntok=ntok: h.activation(out=s_[:, 0:ntok], in_=pg[:, 0:ntok], func=AF.Silu), R=[pg], W=[s_])
                        fw.op("dve", lambda h, pu=pu, s_=s_, hc=hc, ntok=ntok: h.tensor_tensor(out=A[:, hc, 0:ntok], in0=pu[:, 0:ntok], in1=s_[:, 0:ntok], op=ALU.mult),
                              R=[pu, s_], W=[A])
                for m in range(nm):
                    x_ = xb_[m]
                    rows = slice(t0 + m * 128, t0 + (m + 1) * 128)
                    for oc2 in range(2):
                        pb = self.bank()
                        t_ = tm[oc2]
                        for hc in range(22):
                            fw.op("pe", lambda h, pb=pb, hc=hc, m=m, oc2=oc2: h.matmul(
                                pb[:], lhsT=A[:, hc, m * 128:(m + 1) * 128], rhs=W2[:, hc, oc2 * 512:(oc2 + 1) * 512], start=(hc == 0), stop=(hc == 21)),
                                R=[A, W2], W=[pb])
                        fw.op("dve", lambda h, pb=pb, t_=t_, gi=gi, oc2=oc2: h.tensor_tensor(out=t_[:], in0=pb[:], in1=gtB[:, gi, oc2 * 512:(oc2 + 1) * 512], op=ALU.mult),
                              R=[pb, gtB], W=[t_])
                        fw.op("pool", lambda h, t_=t_, x_=x_, oc2=oc2: h.tensor_tensor(out=x_[:, oc2 * 512:(oc2 + 1) * 512], in0=x_[:, oc2 * 512:(oc2 + 1) * 512],
                                                                                      in1=t_[:], op=ALU.add), R=[t_, x_], W=[x_])
                    if not lastl:
                        fw.dma("pool", S["XS"][b][rows, :], x_[:], R=[x_])
                    else:
                        o_ = ob[oc_ % 2]
                        oc_ += 1
                        fw.op("act", lambda h, x_=x_: h.activation(out=junk[:], in_=x_[:], func=AF.Square, accum_out=ss[:, 0:1]), R=[x_], W=[junk, ss])
                        fw.op("act", lambda h: h.activation(out=ss[:, 1:2], in_=ss[:, 0:1], func=AF.Sqrt, bias=self.epsc[:, 0:1], scale=1.0 / D), R=[ss, self.epsc], W=[ss])
                        fw.op("dve", lambda h: h.reciprocal(out=rstd[:], in_=ss[:, 1:2]), R=[ss], W=[rstd])
                        fw.op("dve", lambda h, x_=x_, o_=o_: h.scalar_tensor_tensor(out=o_[:], in0=x_[:], scalar=rstd[:, 0:1], in1=gfB[:], op0=ALU.mult, op1=ALU.mult),
                              R=[x_, rstd, gfB], W=[o_])
                        fw.dma("pool", self.out[b][t0 - TC + m * 128:t0 - TC + (m + 1) * 128, :], o_[:], R=[o_])
            fw.barrier()


def rope_tables(TL):
    TT = TC + TL
    rows = TL // 64
    r, col = np.meshgrid(np.arange(rows), np.arange(64), indexing="ij")
    half = 32
    inv = (10000.0 ** (-np.arange(0, half, 2, dtype=np.float32) / half)).astype(np.float32)
    ang_r = r.reshape(-1, 1).astype(np.float32) * inv
    ang_c = col.reshape(-1, 1).astype(np.float32) * inv
    ang = np.concatenate([ang_r, ang_r, ang_c, ang_c], axis=-1)
    cos = np.cos(ang).astype(np.float32)
    sin = np.sin(ang).astype(np.float32)
    sign = np.where((np.arange(64) % 32) < 16, -1.0, 1.0).astype(np.float32)
    sin = sin * sign[None, :]
    tabs = np.zeros((4, 128, TT), np.float32)
    for p in range(128):
        d = p % 64
        tabs[0, p, :TC] = 0.125
        tabs[2, p, :TC] = 1.0
        tabs[0, p, TC:] = cos[:, d] * 0.125
        tabs[1, p, TC:] = sin[:, d] * 0.125
        tabs[2, p, TC:] = cos[:, d]
        tabs[3, p, TC:] = sin[:, d]
    return tabs


def const_mats():
    i = np.arange(128)
    ident = (i[:, None] == i[None, :]).astype(np.float32)
    triF = (i[:, None] <= i[None, :]).astype(np.float32)
    triB = (i[:, None] >= i[None, :]).astype(np.float32)
    ones = np.ones((128, 128), np.float32)
    return np.ascontiguousarray(np.stack([ident, triF, triB, ones], axis=1))


def make_in_maps(inputs, TL, NB, ncores):
    f = lambda a: np.ascontiguousarray(np.asarray(a, dtype=np.float32))
    x, c, ctx, c_ctx = f(inputs["x"]), f(inputs["c"]), f(inputs["ctx"]), f(inputs["c_ctx"])
    L = inputs["w_in"].shape[0]
    w_in = f(inputs["w_in"])
    w_in_ext = np.zeros((L, D, NEXT_PAD), np.float32)
    w_in_ext[:, :, :NEXT] = w_in[:, :, np.asarray(IN_COLS)]
    w_ffn_in = np.ascontiguousarray(f(inputs["w_ffn_in"])[:, :, np.asarray(FFN_COLS)])
    shared = dict(
        w_mod=f(inputs["w_mod"]), b_mod=f(inputs["b_mod"]), g_mix=f(inputs["g_mix"]), g_ffn=f(inputs["g_ffn"]),
        w_in=w_in_ext, b_gate=f(inputs["b_gate"]), conv_w=f(inputs["conv_w"]).reshape(L, 3 * D), m_norm=f(inputs["m_norm"]),
        sink=f(inputs["sink"]),
        lam=np.ascontiguousarray(np.concatenate([f(inputs["lam_q1"]), f(inputs["lam_k1"]), f(inputs["lam_q2"]), f(inputs["lam_k2"])], axis=1)),
        d_norm=f(inputs["d_norm"]), w_branch=f(inputs["w_branch"]).reshape(L, 1536, D), w_out=f(inputs["w_out"]),
        w_ffn_in=w_ffn_in, w_ffn_out=f(inputs["w_ffn_out"]), g_final=f(inputs["g_final"]),
        tabs=rope_tables(TL), cmat=const_mats(),
    )
    maps = []
    for i in range(ncores):
        bs = slice(i * NB, (i + 1) * NB)
        m = dict(shared)
        m["xin"] = np.ascontiguousarray(np.concatenate([ctx[bs], x[bs]], axis=1))
        m["cond"] = np.ascontiguousarray(np.concatenate([c[bs], c_ctx[None, :]], axis=0))
        maps.append(m)
    return maps


_CACHE = {}


def kernel(**inputs):
    x = np.asarray(inputs["x"])
    B, TL, _ = x.shape
    ncores = NCORES if B % NCORES == 0 else 1
    NB = B // ncores
    key = (TL, NB)
    if key not in _CACHE:
        _CACHE[key] = Prog(TL, NB).build()
    nc = _CACHE[key]
    maps = make_in_maps(inputs, TL, NB, ncores)
    res = run_bass_kernel_spmd(nc, maps, core_ids=list(range(ncores)))
    out = np.concatenate([np.asarray(r["out"], dtype=np.float32) for r in res.results], axis=0)
    return out
```
